# Optimizing a Trainium2 kernel written in Bass

```python
import jax, jax.numpy as jnp
from jax import lax
import numpy as np

D_MODEL = 4096
BATCH = 4
SEQ = 2048
DEPTH = 2

GRID_W = 64
CTX_LEN = 256
N_MIXERS = 2
N_LAYERS_A = (DEPTH + 1) // 2
N_LAYERS_B = DEPTH // 2
NA_HEADS = 32
NA_HEAD_DIM = D_MODEL // NA_HEADS
NA_KH_MAX = 8
NA_KW = 16
RPB_H = 2 * NA_KH_MAX - 1
RPB_W = 2 * NA_KW - 1
SG_CHUNK = 128
SG_WIDTH = D_MODEL
SG_GROUPS = 8
SG_GROUP_DIM = SG_WIDTH // SG_GROUPS
D_FF = 2 * D_MODEL
N_FFN = 2
N_SUB = 3
N_MOD = 3 * N_SUB
EPS = 1e-6
NEG_INF = -1e30

kernel_name = "hybrid_natten_gmlp_macaron_dit"


def rms_norm(x, gain):
    xf = x.astype(jnp.float32)
    y = xf * lax.rsqrt(jnp.mean(xf * xf, axis=-1, keepdims=True) + EPS)
    return (y * gain.astype(jnp.float32)).astype(x.dtype)


def modulate(x, gain, mod, j):
    return rms_norm(x, gain) * (1 + mod[:, 3 * j + 1]) + mod[:, 3 * j]


def swiglu(h, w_in, w_out):
    a, g = jnp.split(h @ w_in, 2, axis=-1)
    return (jax.nn.silu(a) * g) @ w_out


def half_ffn(x, gain, mod, j, w_in, w_out):
    h = modulate(x, gain, mod, j)
    return x + 0.5 * mod[:, 3 * j + 2] * swiglu(h, w_in, w_out)


def neighbourhood_attention(h_lat, h_ctx, w_qkv, w_o, q_gain, k_gain, rpb, with_ctx_out):
    B, S, _ = h_lat.shape
    rows = S // GRID_W
    kh = min(NA_KH_MAX, rows)
    scale = NA_HEAD_DIM ** -0.5

    def project(h):
        qkv = (h @ w_qkv).reshape(h.shape[0], h.shape[1], 3, NA_HEADS, NA_HEAD_DIM)
        q = rms_norm(qkv[:, :, 0], q_gain) * scale
        k = rms_norm(qkv[:, :, 1], k_gain)
        return q, k, qkv[:, :, 2]

    q, k, v = project(h_lat)
    qc, kc, vc = project(h_ctx)

    def to_grid(t):
        return t.reshape(B, rows, GRID_W, NA_HEADS, NA_HEAD_DIM).transpose(1, 0, 3, 2, 4)

    q_g, k_g, v_g = to_grid(q), to_grid(k), to_grid(v)

    col = jnp.arange(GRID_W)
    cs = jnp.clip(col - NA_KW // 2, 0, GRID_W - NA_KW)
    col_ok = (col[None, :] >= cs[:, None]) & (col[None, :] < cs[:, None] + NA_KW)
    dc = jnp.clip(col[None, :] - col[:, None], -(NA_KW - 1), NA_KW - 1) + NA_KW - 1
    rpb_cols = rpb[:, :, dc]
    r_all = jnp.arange(rows)
    rs = jnp.clip(r_all - kh // 2, 0, rows - kh)
    n_lat = kh * GRID_W

    def row_block(args):
        q_r, r, r0 = args
        k_r = lax.dynamic_slice_in_dim(k_g, r0, kh, axis=0)
        v_r = lax.dynamic_slice_in_dim(v_g, r0, kh, axis=0)
        dr = r0 + jnp.arange(kh) - r + NA_KH_MAX - 1
        bias = jnp.take(rpb_cols, dr, axis=1).transpose(0, 2, 1, 3)
        s_lat = jnp.einsum('bhqd,ibhkd->bhqik', q_r, k_r).astype(jnp.float32)
        s_lat = s_lat + bias[None].astype(jnp.float32)
        s_lat = jnp.where(col_ok[:, None, :], s_lat, NEG_INF)
        s_ctx = jnp.einsum('bhqd,bchd->bhqc', q_r, kc).astype(jnp.float32)
        s = jnp.concatenate([s_lat.reshape(B, NA_HEADS, GRID_W, n_lat), s_ctx], axis=-1)
        p = jax.nn.softmax(s, axis=-1).astype(v.dtype)
        p_lat = p[..., :n_lat].reshape(B, NA_HEADS, GRID_W, kh, GRID_W)
        o = (jnp.einsum('bhqik,ibhkd->bqhd', p_lat, v_r)
             + jnp.einsum('bhqc,bchd->bqhd', p[..., n_lat:], vc))
        return o

    o_lat = lax.map(row_block, (q_g, r_all, rs))
    o_lat = o_lat.transpose(1, 0, 2, 3, 4).reshape(B, S, D_MODEL) @ w_o

    o_ctx = None
    if with_ctx_out:
        C = h_ctx.shape[1]
        s_c = jnp.einsum('bqhd,bkhd->bhqk', qc, kc).astype(jnp.float32)
        p_c = jax.nn.softmax(s_c, axis=-1).astype(vc.dtype)
        o_ctx = jnp.einsum('bhqk,bkhd->bqhd', p_c, vc).reshape(B, C, D_MODEL) @ w_o
    return o_lat, o_ctx


def spatial_gating(h, w_in, b_in, v_gain, w_s, b_s, w_out):
    B, L, _ = h.shape
    u, v = jnp.split(jax.nn.gelu(h @ w_in + b_in), 2, axis=-1)
    v = rms_norm(v, v_gain).reshape(B, L // SG_CHUNK, SG_CHUNK, SG_GROUPS, SG_GROUP_DIM)
    mixed = jnp.einsum('gpq,bnqgc->bnpgc', w_s, v) + b_s.T[None, None, :, :, None]
    return (u * mixed.reshape(B, L, SG_WIDTH)) @ w_out


def setup_inputs(seed: int = 0) -> dict:
    key = jax.random.key(seed)
    ks = jax.random.split(key, 20)

    def nrm(k, shape, s):
        return jax.random.normal(k, shape, jnp.float32) * s

    D = D_MODEL
    return {
        "x": nrm(ks[0], (BATCH, SEQ, D), 1.0),
        "c": nrm(ks[1], (BATCH, D), 1.0),
        "ctx": nrm(ks[2], (BATCH, CTX_LEN, D), 1.0),
        "c_ctx": nrm(ks[3], (D,), 1.0),
        "w_ada": nrm(ks[4], (DEPTH, D, N_MOD * D), 0.5 * D ** -0.5),
        "b_ada": nrm(ks[5], (DEPTH, N_MOD * D), 0.02),
        "norm_g": 1.0 + nrm(ks[6], (DEPTH, N_SUB, D), 0.02),
        "ffn_w_in": nrm(ks[7], (DEPTH, N_FFN, D, 2 * D_FF), D ** -0.5),
        "ffn_w_out": nrm(ks[8], (DEPTH, N_FFN, D_FF, D), D_FF ** -0.5),
        "na_w_qkv": nrm(ks[9], (N_LAYERS_A, D, 3 * D), D ** -0.5),
        "na_w_o": nrm(ks[10], (N_LAYERS_A, D, D), D ** -0.5),
        "na_q_gain": 1.0 + nrm(ks[11], (N_LAYERS_A, NA_HEAD_DIM), 0.02),
        "na_k_gain": 1.0 + nrm(ks[12], (N_LAYERS_A, NA_HEAD_DIM), 0.02),
        "na_rpb": nrm(ks[13], (N_LAYERS_A, NA_HEADS, RPB_H, RPB_W), 0.5),
        "sg_w_in": nrm(ks[14], (N_LAYERS_B, D, 2 * SG_WIDTH), D ** -0.5),
        "sg_b_in": nrm(ks[15], (N_LAYERS_B, 2 * SG_WIDTH), 0.02),
        "sg_v_gain": 1.0 + nrm(ks[16], (N_LAYERS_B, SG_WIDTH), 0.02),
        "sg_w_s": nrm(ks[17], (N_LAYERS_B, SG_GROUPS, SG_CHUNK, SG_CHUNK), SG_CHUNK ** -0.5),
        "sg_b_s": 1.0 + nrm(ks[18], (N_LAYERS_B, SG_GROUPS, SG_CHUNK), 0.02),
        "sg_w_out": nrm(ks[19], (N_LAYERS_B, SG_WIDTH, D), SG_WIDTH ** -0.5),
    }


def reference(x, c, ctx, c_ctx, w_ada, b_ada, norm_g, ffn_w_in, ffn_w_out,
              na_w_qkv, na_w_o, na_q_gain, na_k_gain, na_rpb,
              sg_w_in, sg_b_in, sg_v_gain, sg_w_s, sg_b_s, sg_w_out):
    B = x.shape[0]
    z = ctx
    for i in range(DEPTH):
        kind = i % N_MIXERS
        j = i // N_MIXERS
        keep_ctx = i < DEPTH - 1
        ctx_in = keep_ctx or kind == 0
        m_x = (jax.nn.silu(c) @ w_ada[i] + b_ada[i]).reshape(B, N_MOD, 1, D_MODEL)
        m_z = (jax.nn.silu(c_ctx)[None] @ w_ada[i] + b_ada[i]).reshape(1, N_MOD, 1, D_MODEL)

        x = half_ffn(x, norm_g[i, 0], m_x, 0, ffn_w_in[i, 0], ffn_w_out[i, 0])
        if ctx_in:
            z = half_ffn(z, norm_g[i, 0], m_z, 0, ffn_w_in[i, 0], ffn_w_out[i, 0])

        hx = modulate(x, norm_g[i, 1], m_x, 1)
        if kind == 0:
            hz = modulate(z, norm_g[i, 1], m_z, 1)
            ox, oz = neighbourhood_attention(hx, hz, na_w_qkv[j], na_w_o[j], na_q_gain[j],
                                             na_k_gain[j], na_rpb[j], keep_ctx)
        else:
            sg = (sg_w_in[j], sg_b_in[j], sg_v_gain[j], sg_w_s[j], sg_b_s[j], sg_w_out[j])
            ox = spatial_gating(hx, *sg)
            oz = spatial_gating(modulate(z, norm_g[i, 1], m_z, 1), *sg) if keep_ctx else None
        x = x + m_x[:, 5] * ox

        x = half_ffn(x, norm_g[i, 2], m_x, 2, ffn_w_in[i, 1], ffn_w_out[i, 1])
        if keep_ctx:
            z = z + m_z[:, 5] * oz
            z = half_ffn(z, norm_g[i, 2], m_z, 2, ffn_w_in[i, 1], ffn_w_out[i, 1])
    return x
```

```python
from contextlib import ExitStack

import numpy as np
import concourse.bass as bass
import concourse.mybir as mybir
from concourse.bass_utils import run_bass_kernel_spmd

F32 = mybir.dt.float32
BF16 = mybir.dt.bfloat16
AF = mybir.ActivationFunctionType
ALU = mybir.AluOpType
AX = mybir.AxisListType

NW = 4
NCORES = 8
D = 4096
T = 512
KLOC = 1792
NCLS = 5


class DSem:
    def __init__(self, h):
        self.h = h
        self.cnt = 0


class Rec:
    ENG = ["sync", "gpsimd", "tensor", "vector", "scalar"]

    def __init__(self, nc, es):
        self.nc = nc
        self.es = es
        self.ops = {e: [] for e in self.ENG}
        self.psem = {e: es.enter_context(nc.semaphore("p_" + e)) for e in self.ENG}
        self.pcnt = {e: 0 for e in self.ENG}
        self.seen = {e: {} for e in self.ENG}
        self.last = {e: None for e in self.ENG}
        self.nsem = 0
        self.dsems = []

    def dsem(self, name=None):
        self.nsem += 1
        d = DSem(self.es.enter_context(self.nc.semaphore("d%d" % self.nsem)))
        self.dsems.append(d)
        return d

    def op(self, eng, fn, waits=(), sig=None, dsem=None):
        if sig is None:
            sig = eng in ("vector", "scalar")
        w = []
        flat = []

        def _fl(x):
            if x is None:
                return
            if isinstance(x, list):
                for y in x:
                    _fl(y)
            else:
                flat.append(x)
        _fl(list(waits))
        for tok in flat:
            sem, val = tok
            key = id(sem)
            if self.seen[eng].get(key, 0) >= val:
                continue
            self.seen[eng][key] = val
            w.append((sem, val))
        tok = None
        inc = None
        if dsem is not None:
            dsem.cnt += 16
            inc = (dsem.h, 16)
            tok = (dsem.h, dsem.cnt)
        elif sig:
            self.pcnt[eng] += 1
            inc = (self.psem[eng], 1)
            tok = (self.psem[eng], self.pcnt[eng])
            self.last[eng] = tok
        self.ops[eng].append((w, fn, inc))
        return tok

    def replay(self, block):
        for e in self.ENG:
            ops = self.ops[e]

            def body(eng, ops=ops):
                for (w, fn, inc) in ops:
                    for sem, val in w:
                        eng.wait_ge(sem, val)
                    if fn is None:
                        continue
                    ins = fn(eng)
                    if inc is not None:
                        ins.then_inc(inc[0], inc[1])

            getattr(block, e)(body)


def build(dbg=None):
    nc = bass.Bass("TRN2", target_bir_lowering=False)
    es = ExitStack()
    R = Rec(nc, es)

    def din(name, shape, dt=F32):
        return nc.dram_tensor(name, shape, dt, kind="ExternalInput").ap()

    def dscr(name, shape, dt):
        return nc.dram_tensor(name, shape, dt, kind=("ExternalOutput" if dbg else "Internal")).ap()

    def sb(name, shape, dt):
        return es.enter_context(nc.sbuf_tensor(name, shape, dt))

    xT = din("xT", [D, 1536])
    cs_in = din("cs", [128, 64])
    w_ada = din("w_ada", [2, D, 9 * D])
    b_ada = din("b_ada", [128, 2 * 288])
    norm_g = din("norm_g", [128, 6 * 32])
    ffn_w_in = din("ffn_w_in", [2, 2, D, 4 * D])
    ffn_w_out = din("ffn_w_out", [2, 2, 2 * D, D])
    w_qkv = din("w_qkv", [D, 3 * D])
    w_o = din("w_o", [D, D])
    qk_gain = din("qk_gain", [128, 2])
    bias_in = din("bias", [32, 128, NCLS * 768])
    sg_w_in = din("sg_w_in", [D, 2 * D])
    sg_w_out = din("sg_w_out", [D, D])
    sg_bu = din("sg_bu", [128, 32])
    sg_bv = din("sg_bv", [1, D])
    sg_vg = din("sg_vg", [128, 32])
    sg_wsT = din("sg_wsT", [128, 1024])
    sg_bs = din("sg_bs", [1024])
    outT = nc.dram_tensor("outT", [D, 1024], F32, kind="ExternalOutput").ap()

    X1 = dscr("X1s", [D, 1024], F32)
    Qs = dscr("Qs", [32, 128, 1024], BF16)
    Ks = dscr("Ks", [32, 128, KLOC], BF16)
    Vs = dscr("Vs", [KLOC, D], BF16)
    Os = dscr("Os", [32, 128, 1024], BF16)
    if dbg:
        MODo = nc.dram_tensor("MODo", [128, 3 * 384], F32, kind="ExternalOutput").ap()

    XTt = sb("XT", [128, 32 * T], F32)
    HTt = sb("HT", [128, 32 * T], BF16)
    UGt = sb("UG", [128, 32 * T], BF16)
    WBt = sb("WB", [128, NW * 16 * 256], BF16)
    XT = XTt[:, :].rearrange("p (k t) -> p k t", k=32)
    HT = HTt[:, :].rearrange("p (k t) -> p k t", k=32)
    UG = UGt[:, :].rearrange("p (k t) -> p k t", k=32)
    WB = WBt[:, :].rearrange("p (s k n) -> p s k n", s=NW, k=16)

    SHt = sb("SH", [128, 384], F32)
    GSt = sb("GS", [128, 384], F32)
    GHt = sb("GH", [128, 384], F32)
    NGt = sb("NG", [128, 192], F32)
    CSF = sb("CSF", [128, 64], F32)
    SC = sb("SC", [128, 64], BF16)
    ONESB = sb("ONESB", [128, 128], BF16)
    ONESD = sb("ONESD", [128, 128], BF16)
    ONESH = sb("ONESH", [128, 128], F32)
    EPS = sb("EPS", [128, 1], F32)
    QKG = sb("QKG", [128, 2], F32)
    SAt = sb("SA", [128, 2 * T], F32)
    BU = sb("BU", [128, 32], F32)
    VG = sb("VG", [128, 32], F32)
    WSTt = sb("WST", [128, 1024], BF16)
    BSB = sb("BSB", [128, 1024], F32)
    BADAt = sb("BADA", [128, 576], F32)
    TMt = sb("TM", [128, 64], F32)
    PS = [es.enter_context(nc.psum_tensor("ps%d" % i, [128, T], F32)) for i in range(8)]

    def mvec(t, l, j, s, kc):
        o = ((l * 3 + j) * 2 + s) * 32 + kc
        return t[:, o:o + 1]

    def carve(region, off_bytes, nbytes, dt):
        a = region[:, off_bytes // 4:(off_bytes + nbytes) // 4]
        return a if dt == F32 else a.bitcast(dt)

    XR = XTt[:, :]
    UR = UGt[:, :].bitcast(F32)

    SQ = [carve(UR, 0, 1024, BF16), carve(UR, 1024, 1024, BF16)]
    RSTD = carve(UR, 2048, 2048, F32)
    TMP = [carve(UR, 4096, 2048, F32), carve(UR, 6144, 2048, F32)]
    SQF = [carve(UR, 8192, 2048, F32), carve(UR, 10240, 2048, F32)]
    RR = [carve(UR, 12288, 2048, F32), carve(UR, 14336, 2048, F32)]
    QST = [carve(UR, 16384, 1024, BF16), carve(UR, 17408, 1024, BF16)]
    VST = [carve(UR, 18432, 2048, BF16).rearrange("p (c n) -> p c n", c=4),
           carve(UR, 20480, 2048, BF16).rearrange("p (c n) -> p c n", c=4)]

    SA = [SAt[:, 0:T], SAt[:, T:2 * T]]
    WST = WSTt[:, :].rearrange("p (g n) -> p g n", g=8)

    st = {"wn": 0, "psn": 0, "ada_pos": 0}
    wfree = [None] * NW
    wsem = [R.dsem() for _ in range(NW)]
    psfree = [None] * 8
    setup_sem = R.dsem()

    def new_wsems():
        for i in range(NW):
            wsem[i] = R.dsem()

    def wload(parts):
        s = st["wn"] % NW
        st["wn"] += 1
        tok = None
        for i, (dst, src) in enumerate(parts):
            d = dst(s)
            tok = R.op("gpsimd", lambda e, d=d, src=src: e.dma_start(out=d, in_=src),
                       waits=[wfree[s]] if i == 0 else [], dsem=wsem[s])
        return s, tok

    def ps_get():
        nb = 6 if st["ada_pos"] < 288 else 8
        b = st["psn"] % nb
        st["psn"] += 1
        return b

    def wv(W):
        return W.rearrange("(kc p) n -> p kc n", p=128)

    def load_halves(W, kc0, c0, c1):
        Wv = wv(W)
        halves = []
        for half in range(2):
            k0 = kc0 + half * 16
            parts = []
            if c1 == c0 + 128:
                for q in range(2):
                    parts.append((lambda s, q=q: WB[:, s, q * 8:(q + 1) * 8, :],
                                  Wv[:, k0 + q * 8:k0 + q * 8 + 8, c0:c0 + 256]))
            else:
                parts.append((lambda s: WB[:, s, :, 0:128], Wv[:, k0:k0 + 16, c0:c0 + 128]))
                parts.append((lambda s: WB[:, s, :, 128:256], Wv[:, k0:k0 + 16, c1:c1 + 128]))
            halves.append(wload(parts))
        return halves

    def unit_ws(W, kc0, c0, c1, rhs_fn, rhs_tok, ncols=T, out_cols=None, banks=None):
        halves = load_halves(W, kc0, c0, c1)
        if banks is None:
            banks = [ps_get(), ps_get()]
        done = [None, None]
        for half in range(2):
            s, wtok = halves[half]
            for ch in range(2):
                b = banks[ch]
                for k in range(16):
                    kc = half * 16 + k
                    first = (half == 0 and k == 0)
                    last = (half == 1 and k == 15)
                    waits = []
                    if k == 0 and ch == 0:
                        waits.append(wtok)
                    if first:
                        waits += [psfree[b], rhs_tok]
                    sig = last or (ch == 1 and k == 15)
                    o = PS[b][:, 0:ncols] if out_cols is None else PS[b][:, out_cols[0]:out_cols[1]]
                    tok = R.op("tensor",
                               lambda e, o=o, l=WB[:, s, k, ch * 128:(ch + 1) * 128], r=rhs_fn(kc), f=first, la=last:
                               e.matmul(o, lhsT=l, rhs=r, start=f, stop=la),
                               waits=waits, sig=sig)
                    if ch == 1 and k == 15:
                        wfree[s] = tok
                    if last:
                        done[ch] = tok
        return [(banks[0], done[0]), (banks[1], done[1])]

    def unit_tm(W, c0, lhs_fn, lhs_tok, bias_row=None, bias_tok=None):
        halves = load_halves(W, 0, c0, c0 + 128)
        banks = [ps_get() for _ in range(4)]
        done = [None] * 4
        for half in range(2):
            s, wtok = halves[half]
            for tc in range(4):
                b = banks[tc]
                for k in range(16):
                    kc = half * 16 + k
                    first = (half == 0 and k == 0)
                    last = (half == 1 and k == 15)
                    waits = []
                    if k == 0 and tc == 0:
                        waits.append(wtok)
                    if first:
                        waits += [psfree[b], lhs_tok]
                    stop = last and bias_row is None
                    sig = (last and bias_row is None) or (tc == 3 and k == 15)
                    tok = R.op("tensor",
                               lambda e, o=PS[b][:, 0:256], l=lhs_fn(kc, tc), r=WB[:, s, k, :], f=first, la=stop:
                               e.matmul(o, lhsT=l, rhs=r, start=f, stop=la),
                               waits=waits, sig=sig)
                    if tc == 3 and k == 15:
                        wfree[s] = tok
                    if last and bias_row is None:
                        done[tc] = tok
        if bias_row is not None:
            for tc in range(4):
                b = banks[tc]
                done[tc] = R.op("tensor",
                                lambda e, o=PS[b][:, 0:256], r=bias_row:
                                e.matmul(o, lhsT=ONESB[0:1, 0:128], rhs=r, start=False, stop=True),
                                waits=[bias_tok] if tc == 0 else [], sig=True)
        return [(banks[tc], done[tc]) for tc in range(4)]

    def s_dma(out, in_):
        return R.op("sync", lambda e: e.dma_start(out=out, in_=in_), dsem=setup_sem)

    s_dma(CSF[:, :], cs_in)
    s_dma(NGt[:, :], norm_g)
    s_dma(BADAt[:, :], b_ada)
    s_dma(QKG[:, :], qk_gain)
    s_dma(BU[:, :], sg_bu)
    s_dma(VG[:, :], sg_vg)
    tok_setup = s_dma(BSB[:, :], sg_bs.partition_broadcast(128))
    wst_sem = R.dsem()
    R.op("gpsimd", lambda e: e.memset(ONESB[:, :], 1.0))
    R.op("gpsimd", lambda e: e.memset(ONESD[:, :], 1.0 / D))
    R.op("gpsimd", lambda e: e.memset(ONESH[:, :], 1.0 / 128))
    tok_ms = R.op("gpsimd", lambda e: e.memset(EPS[:, :], 1e-6), sig=True)
    tok_wst = R.op("gpsimd", lambda e: e.dma_start(out=WSTt[:, :], in_=sg_wsT), dsem=wst_sem)

    tok_sc = R.op("scalar", lambda e: e.activation(out=SC[:, :], in_=CSF[:, :], func=AF.Silu),
                  waits=[tok_setup])
    R.op("vector", lambda e: e.tensor_scalar(out=QKG[:, 0:1], in0=QKG[:, 0:1], scalar1=float(128 ** -0.5),
                                              scalar2=None, op0=ALU.mult), waits=[tok_setup])

    SCv = SC[:, :].rearrange("p (k s) -> p k s", s=2)
    NG = NGt[:, :].rearrange("p (a k) -> p a k", k=32)
    modtok = {}

    def ada_unit():
        pos = st["ada_pos"]
        if pos >= 288:
            return
        st["ada_pos"] += 1
        l, rem = divmod(pos, 144)
        m, uu = divmod(rem, 16)
        gm = l * 9 + m
        b = 6 + (gm % 2)
        c0 = (m * 16 + uu) * 256
        halves = load_halves(w_ada[l], 0, c0, c0 + 128)
        tok = None
        for ch in range(2):
            col = 2 * (2 * uu + ch)
            for kc in range(32):
                s_, wtok = halves[kc // 16]
                k = kc % 16
                waits = []
                if ch == 0 and k == 0:
                    waits.append(wtok)
                if ch == 0 and kc == 0:
                    waits += [tok_sc, psfree[b] if uu == 0 else None]
                sig = (ch == 1 and k == 15)
                tok = R.op("tensor", lambda e, o=PS[b][:, col:col + 2], l_=WB[:, s_, k, ch * 128:(ch + 1) * 128], r=SCv[:, kc, :],
                           f=(kc == 0), la=(kc == 31): e.matmul(o, lhsT=l_, rhs=r, start=f, stop=la),
                           waits=waits, sig=sig)
                if sig:
                    wfree[s_] = tok
        if uu == 15:
            j, kind = divmod(m, 3)
            dd = None
            for s2 in range(2):
                o = ((l * 3 + j) * 2 + s2) * 32
                psv = PS[b][:, 0:64].rearrange("p (k s) -> p k s", s=2)[:, :, s2]
                bada = BADAt[:, l * 288 + m * 32:l * 288 + m * 32 + 32]
                tm = TMt[:, s2 * 32:(s2 + 1) * 32]
                if kind == 0:
                    dd = R.op("vector", lambda e, o=o, psv=psv, bada=bada: e.tensor_tensor(out=SHt[:, o:o + 32], in0=psv, in1=bada, op=ALU.add),
                              waits=[tok, tok_setup, R.last["vector"]])
                elif kind == 1:
                    R.op("vector", lambda e, psv=psv, bada=bada, tm=tm: e.scalar_tensor_tensor(out=tm, in0=psv, scalar=1.0, in1=bada,
                                                                                           op0=ALU.add, op1=ALU.add),
                         waits=[tok, tok_setup, R.last["vector"]])
                    dd = R.op("vector", lambda e, o=o, tm=tm, l=l, j=j: e.tensor_tensor(out=GSt[:, o:o + 32], in0=tm, in1=NG[:, l * 3 + j, :], op=ALU.mult),
                              waits=[R.last["vector"]])
                else:
                    R.op("vector", lambda e, psv=psv, bada=bada, tm=tm: e.tensor_tensor(out=tm, in0=psv, in1=bada, op=ALU.add),
                         waits=[tok, tok_setup, R.last["vector"]])
                    dd = R.op("vector", lambda e, o=o, tm=tm, j=j: e.tensor_scalar(out=GHt[:, o:o + 32], in0=tm, scalar1=(1.0 if j == 1 else 0.5),
                                                                                 scalar2=None, op0=ALU.mult),
                              waits=[R.last["vector"]])
            psfree[b] = dd
            modtok[(l, m)] = dd

    def ada_units(n):
        for _ in range(n):
            ada_unit()

    ada_units(48)
    tok_mods = tok_setup

    out_sem = R.dsem()
    if dbg:
        R.op("sync", lambda e: e.dma_start(out=MODo[:, 0:384], in_=SHt[:, :]), waits=[R.last["vector"]], dsem=out_sem)
        R.op("sync", lambda e: e.dma_start(out=MODo[:, 384:768], in_=GSt[:, :]), dsem=out_sem)
        R.op("sync", lambda e: e.dma_start(out=MODo[:, 768:1152], in_=GHt[:, :]), dsem=out_sem)

    buf = {"xt_ready": None, "ht_free": None, "ht_ready": None, "ug_free": tok_mods, "xt_free": None}
    x1_sem = R.dsem()

    def norm_mod(l, j, segs):
        bN = ps_get()
        pe_tok = [None, None]
        act_tok = None
        for kc in range(32):
            a = R.op("scalar", lambda e, o=SQ[kc % 2], i=XT[:, kc, :]: e.activation(out=o, in_=i, func=AF.Square),
                     waits=[buf["xt_ready"], pe_tok[kc % 2], buf["ug_free"]])
            pe_tok[kc % 2] = R.op("tensor", lambda e, r=SQ[kc % 2], f=(kc == 0), la=(kc == 31):
                                  e.matmul(PS[bN][:, :], lhsT=ONESD[:, :], rhs=r, start=f, stop=la),
                                  waits=[a, psfree[bN] if kc == 0 else None], sig=True)
        a = R.op("scalar", lambda e: e.activation(out=RSTD, in_=PS[bN][:, :], func=AF.Sqrt, bias=EPS[:, 0:1], scale=1.0),
                 waits=[pe_tok[1]])
        psfree[bN] = a
        d = R.op("vector", lambda e: e.reciprocal(out=RSTD, in_=RSTD), waits=[a])
        atoks = [None, None]
        for kc in range(32):
            d = R.op("vector", lambda e, o=TMP[kc % 2], i=XT[:, kc, :]: e.tensor_tensor(out=o, in0=i, in1=RSTD, op=ALU.mult),
                     waits=[d if kc == 0 else None, atoks[kc % 2], buf["xt_ready"]])
            for (c0, c1, s) in segs:
                atoks[kc % 2] = R.op("scalar", lambda e, o=HT[:, kc, c0:c1], i=TMP[kc % 2][:, c0:c1],
                                     sc=mvec(GSt, l, j, s, kc), bi=mvec(SHt, l, j, s, kc):
                                     e.activation(out=o, in_=i, func=AF.Identity, bias=bi, scale=sc),
                                     waits=[d, buf["ht_free"], modtok[(l, 3 * j)], modtok[(l, 3 * j + 1)]])
        buf["ht_ready"] = R.last["scalar"]
        return buf["ht_ready"]

    def ffn(l, f, segs, n_ada=0):
        j = 0 if f == 0 else 2
        Win = ffn_w_in[l, f]
        Wout = ffn_w_out[l, f]
        mu = {"i": 0}

        def after_main():
            i = mu["i"]
            mu["i"] += 1
            if ((i + 1) * n_ada) // 96 > (i * n_ada) // 96:
                ada_unit()
        for g in range(2):
            sa_free = [None, None]
            ug_ready = None
            for jj in range(32):
                jg = g * 32 + jj
                res = unit_ws(Win, 0, jg * 128, 2 * D + jg * 128, lambda kc: HT[:, kc, :], buf["ht_ready"])
                (bA, tA), (bG, tG) = res
                a = R.op("scalar", lambda e, o=SA[jj % 2], b=bA: e.activation(out=o, in_=PS[b][:, :], func=AF.Silu),
                         waits=[tA, sa_free[jj % 2]])
                psfree[bA] = a
                dd = R.op("vector", lambda e, o=UG[:, jj, :], i0=SA[jj % 2], b=bG: e.tensor_tensor(out=o, in0=i0, in1=PS[b][:, :], op=ALU.mult),
                          waits=[a, tG, buf["ug_free"]])
                psfree[bG] = dd
                sa_free[jj % 2] = dd
                ug_ready = dd
                after_main()
            buf["ht_free"] = R.last["tensor"] if g == 1 else buf["ht_free"]
            for dp in range(16):
                res = unit_ws(Wout, g * 32, dp * 256, dp * 256 + 128, lambda kc: UG[:, kc, :], ug_ready)
                for ch in range(2):
                    b, tk = res[ch]
                    c = dp * 2 + ch
                    dd = None
                    for (c0, c1, s) in segs:
                        dd = R.op("vector", lambda e, o=XT[:, c, c0:c1], b=b, sc=mvec(GHt, l, j, s, c), c0=c0, c1=c1:
                                  e.scalar_tensor_tensor(out=o, in0=PS[b][:, c0:c1], scalar=sc, in1=o, op0=ALU.mult, op1=ALU.add),
                                  waits=[tk, buf["xt_ready"], modtok[(l, 3 * j + 2)]])
                    psfree[b] = dd
                after_main()
            buf["ug_free"] = R.last["tensor"]
        buf["xt_ready"] = R.last["vector"]

    def load_xt(src2d, col0, sem):
        v = src2d.rearrange("(kc p) t -> p kc t", p=128)
        tok = None
        for q in range(4):
            tok = R.op("sync", lambda e, q=q: e.dma_start(out=XT[:, q * 8:(q + 1) * 8, :], in_=v[:, q * 8:(q + 1) * 8, col0:col0 + T]),
                       waits=[buf["xt_free"]] if q == 0 else [], dsem=sem)
        buf["xt_ready"] = tok
        return tok

    def store_xt(dst2d, col0, sem):
        v = dst2d.rearrange("(kc p) t -> p kc t", p=128)
        tok = None
        for q in range(4):
            tok = R.op("sync", lambda e, q=q: e.dma_start(out=v[:, q * 8:(q + 1) * 8, col0:col0 + T], in_=XT[:, q * 8:(q + 1) * 8, :]),
                       waits=[buf["xt_ready"]] if q == 0 else [], dsem=sem)
        return tok

    qst_sem = [R.dsem(), R.dsem()]
    vst_sem = [R.dsem(), R.dsem()]

    def qkv(tile):
        has_q = tile < 2
        kb = 256 + tile * 512
        pend = []
        cnt = {"i": 0, "v": 0}

        def post_head(b, tk, which, hd):
            i = cnt["i"]
            cnt["i"] += 1
            p = i % 2
            a1 = R.op("scalar", lambda e: e.activation(out=SQF[p], in_=PS[b][:, :], func=AF.Square), waits=[tk])
            bn = ps_get()

            def later():
                t1 = R.op("tensor", lambda e: e.matmul(PS[bn][:, :], lhsT=ONESH[:, :], rhs=SQF[p], start=True, stop=True),
                          waits=[a1, psfree[bn]], sig=True)
                a2 = R.op("scalar", lambda e: e.activation(out=RR[p], in_=PS[bn][:, :], func=AF.Sqrt, bias=EPS[:, 0:1], scale=1.0),
                          waits=[t1])
                psfree[bn] = a2
                d1 = R.op("vector", lambda e: e.reciprocal(out=RR[p], in_=RR[p]), waits=[a2])
                d2 = R.op("vector", lambda e: e.scalar_tensor_tensor(out=QST[p], in0=PS[b][:, :], scalar=QKG[:, which:which + 1],
                                                                     in1=RR[p], op0=ALU.mult, op1=ALU.mult),
                          waits=[d1, (qst_sem[p].h, qst_sem[p].cnt)])
                psfree[b] = d2
                if which == 0:
                    R.op("sync", lambda e: e.dma_start(out=Qs[hd][:, tile * T:(tile + 1) * T], in_=QST[p]), waits=[d2], dsem=qst_sem[p])
                elif tile < 2:
                    R.op("sync", lambda e: e.dma_start(out=Ks[hd][:, kb:kb + T], in_=QST[p]), waits=[d2], dsem=qst_sem[p])
                else:
                    R.op("sync", lambda e: e.dma_start(out=Ks[hd][:, 0:256], in_=QST[p][:, 0:256]), waits=[d2], dsem=qst_sem[p])
                    R.op("sync", lambda e: e.dma_start(out=Ks[hd][:, 1280:1536], in_=QST[p][:, 0:256]), dsem=qst_sem[p])
                    R.op("sync", lambda e: e.dma_start(out=Ks[hd][:, 1536:1792], in_=QST[p][:, 256:512]), dsem=qst_sem[p])
            return later

        def flush():
            while pend:
                pend.pop(0)()

        for which in ((0, 1) if has_q else (1,)):
            for hp in range(16):
                res = unit_ws(w_qkv, 0, which * D + hp * 256, which * D + hp * 256 + 128, lambda kc: HT[:, kc, :], buf["ht_ready"])
                flush()
                for ch in range(2):
                    pend.append(post_head(res[ch][0], res[ch][1], which, hp * 2 + ch))
        for cb in range(16):
            res = unit_tm(w_qkv, 2 * D + cb * 256, lambda kc, tc: HT[:, kc, tc * 128:(tc + 1) * 128], buf["ht_ready"])
            flush()
            i = cnt["v"]
            cnt["v"] += 1
            p = i % 2
            dd = None
            for tc in range(4):
                b, tk = res[tc]
                if tc % 2 == 0:
                    dd = R.op("vector", lambda e, b=b, tc=tc: e.tensor_copy(out=VST[p][:, tc, :], in_=PS[b][:, 0:256]),
                              waits=[tk, (vst_sem[p].h, vst_sem[p].cnt)])
                else:
                    dd = R.op("scalar", lambda e, b=b, tc=tc: e.copy(out=VST[p][:, tc, :], in_=PS[b][:, 0:256]),
                              waits=[tk, (vst_sem[p].h, vst_sem[p].cnt)])
                psfree[b] = dd
            w2 = [R.last["vector"], R.last["scalar"]]
            cols = slice(cb * 256, (cb + 1) * 256)
            if tile < 2:
                R.op("sync", lambda e, cols=cols: e.dma_start(out=Vs[kb:kb + T, cols].rearrange("(c p) n -> p c n", p=128), in_=VST[p]),
                     waits=w2, dsem=vst_sem[p])
            else:
                R.op("sync", lambda e, cols=cols: e.dma_start(out=Vs[0:256, cols].rearrange("(c p) n -> p c n", p=128), in_=VST[p][:, 0:2, :]),
                     waits=w2, dsem=vst_sem[p])
                R.op("sync", lambda e, cols=cols: e.dma_start(out=Vs[1280:1536, cols].rearrange("(c p) n -> p c n", p=128), in_=VST[p][:, 0:2, :]),
                     dsem=vst_sem[p])
                R.op("sync", lambda e, cols=cols: e.dma_start(out=Vs[1536:1792, cols].rearrange("(c p) n -> p c n", p=128), in_=VST[p][:, 2:4, :]),
                     dsem=vst_sem[p])
        flush()
        buf["ht_free"] = R.last["tensor"]
        buf["ug_free"] = [(qst_sem[0].h, qst_sem[0].cnt), (qst_sem[1].h, qst_sem[1].cnt),
                          (vst_sem[0].h, vst_sem[0].cnt), (vst_sem[1].h, vst_sem[1].cnt),
                          R.last["vector"], R.last["scalar"], R.last["tensor"]]

    segs_own = [(0, T, 0)]
    segs_ext = [(0, 256, 0), (256, T, 1)]
    xld_sem = R.dsem()
    for tile in range(3):
        segs = segs_own if tile < 2 else segs_ext
        load_xt(xT, tile * T, xld_sem)
        norm_mod(0, 0, segs)
        ffn(0, 0, segs, n_ada=(32, 48, 48)[tile])
        st_tok = None
        if tile < 2:
            st_tok = store_xt(X1, tile * T, x1_sem)
        norm_mod(0, 1, segs)
        buf["xt_free"] = [st_tok, R.last["scalar"], R.last["vector"]]
        qkv(tile)
        new_wsems()
    p1_done = [(x1_sem.h, x1_sem.cnt), (qst_sem[0].h, qst_sem[0].cnt), (qst_sem[1].h, qst_sem[1].cnt),
               (vst_sem[0].h, vst_sem[0].cnt), (vst_sem[1].h, vst_sem[1].cnt)]

    def attention():
        att_ld = [R.dsem(), R.dsem()]
        os_sem = [R.dsem(), R.dsem()]
        QH, KH, VH, BI, OH = [], [], [], [], []
        for hb in range(2):
            base = hb * 26624
            QH.append(carve(XR, base, 2048, BF16))
            KH.append(carve(XR, base + 2048, 3584, BF16))
            VH.append(carve(XR, base + 5632, 3584, BF16).rearrange("p (c d) -> p c d", c=14))
            BI.append(carve(XR, base + 9216, 15360, F32))
            OH.append(carve(XR, base + 24576, 2048, BF16))
        EX, PT, RD = [], [], []
        for nb in range(2):
            base = 53248 + nb * 5632
            EX.append(carve(XR, base, 3072, F32))
            PT.append(carve(XR, base + 3072, 2048, BF16))
            RD.append(carve(XR, base + 5120, 512, F32))
        head_pe_last = [None, None]
        head_dve_last = [None, None]
        ex_free = [None, None]
        pt_free = [None, None]
        os_tok = [None, None]
        pend = [None]
        n = 0
        for hd in range(32):
            hb = hd % 2
            w0 = [head_pe_last[hb], head_dve_last[hb], os_tok[hb]]
            if hd < 2:
                w0 += [buf["xt_free"], p1_done]
            R.op("sync", lambda e, hb=hb, hd=hd: e.dma_start(out=QH[hb], in_=Qs[hd]), waits=w0, dsem=att_ld[hb])
            R.op("sync", lambda e, hb=hb, hd=hd: e.dma_start(out=KH[hb], in_=Ks[hd]), dsem=att_ld[hb])
            R.op("sync", lambda e, hb=hb, hd=hd: e.dma_start(
                out=VH[hb], in_=Vs[:, hd * 128:(hd + 1) * 128].rearrange("(c p) d -> p c d", p=128)), dsem=att_ld[hb])
            ld_tok = R.op("sync", lambda e, hb=hb, hd=hd: e.dma_start(out=BI[hb], in_=bias_in[hd]), dsem=att_ld[hb])
            e_tok = R.op("scalar", lambda e, hb=hb: e.activation(out=BI[hb], in_=BI[hb], func=AF.Exp), waits=[ld_tok])
            for i in range(8):
                nb = n % 2
                sb0, sb1, ob = (0, 1, 4) if nb == 0 else (2, 3, 5)
                db = ob
                cls = {0: 0, 1: 1, 6: 3, 7: 4}.get(i, 2)
                cs = min(max(i - 1, 0), 6)
                kidxs = [cs + c if c < 6 else 12 + (c - 6) for c in range(8)]
                s_tok = None
                for c in range(8):
                    b = sb0 if c < 4 else sb1
                    col = (c % 4) * 128
                    s_tok = R.op("tensor", lambda e, b=b, col=col, hb=hb, k=kidxs[c], i=i: e.matmul(
                        PS[b][:, col:col + 128], lhsT=KH[hb][:, k * 128:(k + 1) * 128], rhs=QH[hb][:, i * 128:(i + 1) * 128],
                        start=True, stop=True),
                        waits=[ld_tok if c == 0 else None, psfree[b] if c in (0, 4) else None], sig=(c == 7))
                if pend[0] is not None:
                    pend[0]()
                    pend[0] = None
                a1 = R.op("scalar", lambda e, nb=nb, b=sb0: e.activation(out=EX[nb][:, 0:512], in_=PS[b][:, :], func=AF.Exp),
                          waits=[s_tok, ex_free[nb], e_tok])
                psfree[sb0] = a1
                a2 = R.op("scalar", lambda e, nb=nb, b=sb1: e.activation(out=EX[nb][:, 512:768], in_=PS[b][:, 0:256], func=AF.Exp))
                a3 = R.op("scalar", lambda e, nb=nb, b=sb1: e.activation(out=PT[nb][:, 768:1024], in_=PS[b][:, 256:512], func=AF.Exp),
                          waits=[pt_free[nb]])
                psfree[sb1] = a3
                d1 = R.op("vector", lambda e, nb=nb, hb=hb, cls=cls: e.tensor_tensor(
                    out=PT[nb][:, 0:768], in0=EX[nb][:, :], in1=BI[hb][:, cls * 768:(cls + 1) * 768], op=ALU.mult),
                    waits=[a2, e_tok, pt_free[nb]])
                ex_free[nb] = d1

                def pv(nb=nb, hb=hb, hd=hd, i=i, kidxs=kidxs, ob=ob, db=db, d1=d1, a3=a3):
                    tok = None
                    for c in range(8):
                        tok = R.op("tensor", lambda e, c=c: e.matmul(PS[db][:, 0:128], lhsT=ONESB[:, :], rhs=PT[nb][:, c * 128:(c + 1) * 128],
                                                                     start=(c == 0), stop=(c == 7)),
                                   waits=[d1, a3, psfree[db]] if c == 0 else [])
                    for c in range(8):
                        tok = R.op("tensor", lambda e, c=c: e.matmul(PS[ob][:, 128:256], lhsT=VH[hb][:, kidxs[c], :],
                                                                     rhs=PT[nb][:, c * 128:(c + 1) * 128], start=(c == 0), stop=(c == 7)),
                                   waits=[], sig=(c == 7))
                    pt_free[nb] = tok
                    head_pe_last[hb] = tok
                    d2 = R.op("vector", lambda e: e.reciprocal(out=RD[nb], in_=PS[db][:, 0:128]), waits=[tok])
                    d3 = R.op("vector", lambda e: e.tensor_tensor(out=OH[hb][:, i * 128:(i + 1) * 128], in0=PS[ob][:, 128:256],
                                                                  in1=RD[nb], op=ALU.mult),
                              waits=[d2, os_tok[hb] if i == 0 else None])
                    psfree[ob] = d3
                    head_dve_last[hb] = d3
                    if i == 7:
                        os_tok[hb] = R.op("sync", lambda e: e.dma_start(out=Os[hd], in_=OH[hb]), waits=[d3], dsem=os_sem[hb])
                pend[0] = pv
                n += 1
                if i in (1, 4, 6):
                    ada_unit()
        pend[0]()
        buf["xt_free"] = [os_tok[0], os_tok[1], R.last["tensor"], R.last["vector"], R.last["scalar"]]
        return [os_tok[0], os_tok[1]]

    os_done = attention()

    htld_sem = R.dsem()

    def wo_stage(tile):
        v = Os.rearrange("h d t -> d h t")
        tok = None
        for q in range(4):
            tok = R.op("sync", lambda e, q=q: e.dma_start(out=HT[:, q * 8:(q + 1) * 8, :],
                                                           in_=v[:, q * 8:(q + 1) * 8, tile * T:(tile + 1) * T]),
                       waits=[buf["ht_free"], os_done] if q == 0 else [], dsem=htld_sem)
        buf["ht_ready"] = tok
        for dp in range(16):
            res = unit_ws(w_o, 0, dp * 256, dp * 256 + 128, lambda kc: HT[:, kc, :], buf["ht_ready"])
            for ch in range(2):
                b, tk = res[ch]
                c = dp * 2 + ch
                dd = R.op("vector", lambda e, o=XT[:, c, :], b=b, sc=mvec(GHt, 0, 1, 0, c):
                          e.scalar_tensor_tensor(out=o, in0=PS[b][:, :], scalar=sc, in1=o, op0=ALU.mult, op1=ALU.add),
                          waits=[tk, buf["xt_ready"], modtok[(0, 5)]])
                psfree[b] = dd
        buf["ht_free"] = R.last["tensor"]
        buf["xt_ready"] = R.last["vector"]

    sp_sem = R.dsem()
    bvr_sem = [R.dsem(), R.dsem()]

    def gmlp(tile):
        VV = carve(XR, 0, 32768, BF16).rearrange("p (c n) -> p c n", c=4)
        GT = [carve(XR, 32768 + k * 1024, 1024, F32) for k in range(2)]
        SS = carve(XR, 34816, 256, F32)
        SR = carve(XR, 35072, 32, F32)
        MX = [carve(XR, 35328 + k * 2048, 2048, F32) for k in range(2)]
        GU = [carve(XR, 39424 + k * 2048, 2048, F32) for k in range(2)]
        BVR = [carve(XR, 43520 + k * 512, 512, BF16) for k in range(2)]
        JUNK = carve(XR, 44544, 1024, F32)
        sp_tok = store_xt(X1, tile * T, sp_sem)
        guard = [sp_tok, R.last["scalar"], R.last["vector"]]
        z = R.op("vector", lambda e: e.memset(SS, 0.0), waits=guard)
        bvr_free = [None, None]
        gt_free = [None, None]
        k = 0
        for cb in range(16):
            p = cb % 2
            btok = R.op("gpsimd", lambda e, p=p, cb=cb: e.dma_start(out=BVR[p][0:1, :], in_=sg_bv[0:1, cb * 256:(cb + 1) * 256]),
                        waits=[bvr_free[p], guard], dsem=bvr_sem[p])
            res = unit_tm(sg_w_in, D + cb * 256, lambda kc, tc: HT[:, kc, tc * 128:(tc + 1) * 128], buf["ht_ready"],
                          bias_row=BVR[p][0:1, :], bias_tok=btok)
            bvr_free[p] = R.last["tensor"]
            for tc in range(4):
                b, tk = res[tc]
                g = k % 2
                k += 1
                a1 = R.op("scalar", lambda e, b=b, g=g: e.activation(out=GT[g], in_=PS[b][:, 0:256], func=AF.Gelu_apprx_tanh),
                          waits=[tk, gt_free[g], guard])
                psfree[b] = a1
                a2 = R.op("scalar", lambda e, g=g, tc=tc, cb=cb: e.activation(out=JUNK, in_=GT[g], func=AF.Square,
                                                                          accum_out=SS[:, tc * 16 + cb:tc * 16 + cb + 1]),
                          waits=[a1, z])
                d1 = R.op("vector", lambda e, g=g, tc=tc, cb=cb: e.tensor_copy(out=VV[:, tc, cb * 256:(cb + 1) * 256], in_=GT[g]),
                          waits=[a1])
                gt_free[g] = [a2, d1]
        d = R.op("vector", lambda e: e.tensor_reduce(out=SR[:, 0:4], in_=SS.rearrange("p (c n) -> p c n", c=4), axis=AX.X, op=ALU.add),
                 waits=[R.last["scalar"], R.last["vector"]])
        a = R.op("scalar", lambda e: e.activation(out=SR[:, 0:4], in_=SR[:, 0:4], func=AF.Sqrt, bias=EPS[:, 0:1], scale=1.0 / D),
                 waits=[d])
        d = R.op("vector", lambda e: e.reciprocal(out=SR[:, 0:4], in_=SR[:, 0:4]), waits=[a])
        for tc in range(4):
            d = R.op("vector", lambda e, tc=tc: e.tensor_scalar(out=VV[:, tc, :], in0=VV[:, tc, :], scalar1=SR[:, tc:tc + 1],
                                                               scalar2=None, op0=ALU.mult), waits=[d])
        vv_ready = d
        kk = 0
        mx_free = [None, None]
        gu_free = [None, None]
        for up in range(16):
            res = unit_ws(sg_w_in, 0, up * 256, up * 256 + 128, lambda kc: HT[:, kc, :], buf["ht_ready"])
            for ch in range(2):
                b, tk = res[ch]
                c = up * 2 + ch
                g = c // 4
                bm = ps_get()
                mt = None
                for tc in range(4):
                    mt = R.op("tensor", lambda e, tc=tc, c=c, g=g, bm=bm: e.matmul(
                        PS[bm][:, tc * 128:(tc + 1) * 128], lhsT=VV[:, tc, c * 128:(c + 1) * 128], rhs=WST[:, g, :],
                        start=True, stop=True),
                        waits=[vv_ready, psfree[bm], tok_wst] if tc == 0 else [], sig=(tc == 3))
                q = kk % 2
                kk += 1
                a1 = R.op("scalar", lambda e, b=b, q=q, c=c: e.activation(out=GU[q], in_=PS[b][:, :], func=AF.Gelu_apprx_tanh,
                                                                         bias=BU[:, c:c + 1], scale=1.0),
                          waits=[tk, gu_free[q]])
                psfree[b] = a1
                dd = None
                for tc in range(4):
                    dd = R.op("vector", lambda e, tc=tc, q=q, c=c, g=g, bm=bm: e.scalar_tensor_tensor(
                        out=MX[q][:, tc * 128:(tc + 1) * 128], in0=PS[bm][:, tc * 128:(tc + 1) * 128], scalar=VG[:, c:c + 1],
                        in1=BSB[:, g * 128:(g + 1) * 128], op0=ALU.mult, op1=ALU.add),
                        waits=[mt, mx_free[q]] if tc == 0 else [])
                psfree[bm] = dd
                d3 = R.op("vector", lambda e, q=q, c=c: e.tensor_tensor(out=UG[:, c, :], in0=GU[q], in1=MX[q], op=ALU.mult),
                          waits=[a1, dd, buf["ug_free"]])
                mx_free[q] = d3
                gu_free[q] = d3
        buf["ht_free"] = R.last["tensor"]
        ug_ready = R.last["vector"]
        buf["xt_free"] = [R.last["tensor"], R.last["vector"], R.last["scalar"]]
        load_xt(X1, tile * T, sp_sem)
        for dp in range(16):
            res = unit_ws(sg_w_out, 0, dp * 256, dp * 256 + 128, lambda kc: UG[:, kc, :], ug_ready)
            for ch in range(2):
                b, tk = res[ch]
                c = dp * 2 + ch
                dd = R.op("vector", lambda e, o=XT[:, c, :], b=b, sc=mvec(GHt, 1, 1, 0, c):
                          e.scalar_tensor_tensor(out=o, in0=PS[b][:, :], scalar=sc, in1=o, op0=ALU.mult, op1=ALU.add),
                          waits=[tk, buf["xt_ready"], modtok[(1, 5)]])
                psfree[b] = dd
        buf["ug_free"] = R.last["tensor"]
        buf["xt_ready"] = R.last["vector"]

    x3ld_sem = R.dsem()
    if dbg != "p1" and dbg != "attn":
        for tile in range(2):
            load_xt(X1, tile * T, x3ld_sem)
            wo_stage(tile)
            norm_mod(0, 2, segs_own)
            ffn(0, 1, segs_own, n_ada=(16 if tile == 0 else 0))
            assert st["ada_pos"] == 288
            new_wsems()
            norm_mod(1, 0, segs_own)
            ffn(1, 0, segs_own)
            new_wsems()
            norm_mod(1, 1, segs_own)
            gmlp(tile)
            new_wsems()
            norm_mod(1, 2, segs_own)
            ffn(1, 1, segs_own)
            new_wsems()
            st_tok = store_xt(outT, tile * T, out_sem)
            buf["xt_free"] = [st_tok]

    fin = [(d.h, d.cnt) for d in R.dsems if d.cnt > 0]
    R.op("sync", None, waits=fin)

    with nc.Block() as block:
        R.replay(block)
    return nc, es


def _bias_tables(rpb):
    out = []
    reps = [0, 1, 3, 6, 7]
    for h in range(2):
        tab = np.empty((32, 128, NCLS, 6, 128), np.float32)
        keyp = np.arange(128)
        q = np.arange(128)
        ak, kcol = keyp // 64, keyp % 64
        qq, qcol = q // 64, q % 64
        for ci, i in enumerate(reps):
            cs = min(max(i - 1, 0), 6)
            for c in range(6):
                lr = 2 * (cs + c) + ak
                gk = 16 * h - 4 + lr
                r = 16 * h + 2 * i + qq
                rs = np.clip(r - 4, 0, 24)
                ccs = np.clip(qcol - 8, 0, 48)
                GK = gk[:, None]
                Rr = r[None, :]
                valid = (GK >= 0) & (GK < 32) & (GK >= rs[None, :]) & (GK < rs[None, :] + 8)
                valid &= (kcol[:, None] >= ccs[None, :]) & (kcol[:, None] < ccs[None, :] + 16)
                dr = np.clip(GK - Rr + 7, 0, 14)
                dc = np.clip(kcol[:, None] - qcol[None, :], -15, 15) + 15
                vals = rpb[:, dr, dc]
                tab[:, :, ci, c, :] = np.where(valid[None], vals, np.float32(-1e30))
        out.append(tab.reshape(32, 128, NCLS * 768))
    return out


def _prep(inputs):
    x = np.asarray(inputs["x"], np.float32)
    c = np.asarray(inputs["c"], np.float32)
    ctx = np.asarray(inputs["ctx"], np.float32)
    c_ctx = np.asarray(inputs["c_ctx"], np.float32)
    shared = {
        "w_ada": np.ascontiguousarray(inputs["w_ada"], dtype=np.float32),
        "b_ada": np.ascontiguousarray(np.asarray(inputs["b_ada"], np.float32).reshape(2, 288, 128).transpose(2, 0, 1).reshape(128, 576)),
        "norm_g": np.ascontiguousarray(np.asarray(inputs["norm_g"], np.float32).reshape(6, 32, 128).transpose(2, 0, 1).reshape(128, 192)),
        "ffn_w_in": np.ascontiguousarray(inputs["ffn_w_in"], dtype=np.float32),
        "ffn_w_out": np.ascontiguousarray(inputs["ffn_w_out"], dtype=np.float32),
        "w_qkv": np.ascontiguousarray(np.asarray(inputs["na_w_qkv"], np.float32)[0]),
        "w_o": np.ascontiguousarray(np.asarray(inputs["na_w_o"], np.float32)[0]),
        "qk_gain": np.ascontiguousarray(np.stack([np.asarray(inputs["na_q_gain"], np.float32)[0],
                                                  np.asarray(inputs["na_k_gain"], np.float32)[0]], axis=1)),
        "sg_w_in": np.ascontiguousarray(np.asarray(inputs["sg_w_in"], np.float32)[0]),
        "sg_w_out": np.ascontiguousarray(np.asarray(inputs["sg_w_out"], np.float32)[0]),
        "sg_bu": np.ascontiguousarray(np.asarray(inputs["sg_b_in"], np.float32)[0, :D].reshape(32, 128).T),
        "sg_bv": np.ascontiguousarray(np.asarray(inputs["sg_b_in"], np.float32)[0, D:].reshape(1, D)),
        "sg_vg": np.ascontiguousarray(np.asarray(inputs["sg_v_gain"], np.float32)[0].reshape(32, 128).T),
        "sg_wsT": np.ascontiguousarray(np.asarray(inputs["sg_w_s"], np.float32)[0].transpose(2, 0, 1).reshape(128, 1024)),
        "sg_bs": np.ascontiguousarray(np.asarray(inputs["sg_b_s"], np.float32)[0].reshape(1024)),
    }
    btab = _bias_tables(np.asarray(inputs["na_rpb"], np.float32)[0])
    in_maps = []
    for k in range(NCORES):
        b, h = k // 2, k % 2
        own = x[b, 1024 * h:1024 * (h + 1)]
        halo = x[b, 1024:1280] if h == 0 else x[b, 768:1024]
        xt = np.ascontiguousarray(np.concatenate([own, halo, ctx[b]], axis=0).T)
        cs = np.stack([c[b].reshape(32, 128).T, c_ctx.reshape(32, 128).T], axis=2)
        m = dict(shared)
        m["xT"] = xt
        m["cs"] = np.ascontiguousarray(cs.reshape(128, 64))
        m["bias"] = btab[h]
        in_maps.append(m)
    return in_maps


_CACHE = {}


def kernel(**inputs):
    in_maps = _prep(inputs)
    if "nc" not in _CACHE:
        _CACHE["nc"] = build()
    nc, es = _CACHE["nc"]
    res = run_bass_kernel_spmd(nc, in_maps, core_ids=list(range(NCORES)))
    out = np.empty((4, 2048, D), np.float32)
    for k in range(NCORES):
        b, h = k // 2, k % 2
        out[b, 1024 * h:1024 * (h + 1)] = res.results[k]["outT"].T
    return out
```
